# Optimizing a Trainium2 kernel written in Bass

```python
import jax, jax.numpy as jnp
from jax import lax
import numpy as np

D_MODEL = 1024
BATCH = 2
SEQ = 16384
DEPTH = 4

A_HEADS = 8
A_HEAD_DIM = 64
A_WIDTH = A_HEADS * A_HEAD_DIM
DILATED_PATTERNS = ((128, 1), (512, 4), (2048, 16))
A_HALF_STEPS = 64
A_BLOCK = 64
REL_BUCKETS = 32
REL_MAX_DISTANCE = 1024
B_HEADS = 4
B_KEY_DIM = 128
B_VAL_DIM = 256
B_K_WIDTH = B_HEADS * B_KEY_DIM
B_V_WIDTH = B_HEADS * B_VAL_DIM
B_GATE_RANK = 16
B_GATE_NORMALIZER = 16.0
C_HEADS = 12
C_HEAD_DIM = 128
C_WIDTH = C_HEADS * C_HEAD_DIM
MIX_WIDTH = A_WIDTH + B_V_WIDTH
AB_IN_WIDTH = 3 * A_WIDTH + 2 * B_K_WIDTH + B_V_WIDTH + 2 * B_GATE_RANK + MIX_WIDTH
C_IN_WIDTH = 5 * C_WIDTH
CHUNK = 32
N_EVEN = (DEPTH + 1) // 2
N_ODD = DEPTH // 2
NORM_EPS = 1e-5
DEEPNORM_ALPHA = (2 * DEPTH) ** 0.25
DEEPNORM_BETA = (8 * DEPTH) ** -0.25

kernel_name = 'hybrid_dilated_gla_hgrn2_deepnorm_encoder'


def _split(t, sizes):
    out, start = [], 0
    for n in sizes:
        out.append(t[..., start:start + n])
        start += n
    return out


def _t5_bucket(rel):
    half = REL_BUCKETS // 2
    max_exact = half // 2
    n = np.abs(rel)
    large = max_exact + (np.log(np.maximum(n, 1) / max_exact)
                         / np.log(REL_MAX_DISTANCE / max_exact) * (half - max_exact)).astype(np.int32)
    large = np.minimum(large, half - 1)
    return np.where(rel > 0, half, 0) + np.where(n < max_exact, n, large)


def _dilated_band_attention(q, k, v, rel_bias, dilation):
    Bsz, S, H, E = q.shape
    L = S // dilation
    nb = -(-L // A_BLOCK)
    Lp = nb * A_BLOCK

    def to_sub(t):
        return t.reshape(Bsz, L, dilation, H, E).transpose(0, 2, 1, 3, 4)

    qs, ks, vs = to_sub(q), to_sub(k), to_sub(v)
    qs = jnp.pad(qs, ((0, 0), (0, 0), (0, Lp - L), (0, 0), (0, 0))).reshape(Bsz, dilation, nb, A_BLOCK, H, E)

    def band(t):
        t = jnp.pad(t, ((0, 0), (0, 0), (A_BLOCK, Lp - L + A_BLOCK), (0, 0), (0, 0)))
        t = t.reshape(Bsz, dilation, nb + 2, A_BLOCK, H, E)
        return jnp.concatenate([t[:, :, :-2], t[:, :, 1:-1], t[:, :, 2:]], axis=3)

    kb, vb = band(ks), band(vs)
    qi = np.arange(A_BLOCK)[:, None]
    kj = np.arange(3 * A_BLOCK)[None, :]
    off = kj - A_BLOCK - qi
    key_idx = np.arange(nb)[:, None, None] * A_BLOCK + kj[None] - A_BLOCK
    mask = (np.abs(off) <= A_HALF_STEPS)[None] & (key_idx >= 0) & (key_idx < L)
    bias = jnp.transpose(rel_bias.astype(jnp.float32)[_t5_bucket(off * dilation)], (2, 0, 1))
    s = jnp.einsum('brnqhe,brnkhe->brnhqk', qs, kb) * (E ** -0.5) + bias
    s = jnp.where(mask[None, None, :, None], s, -jnp.inf)
    lse = jax.nn.logsumexp(s, axis=-1)
    p = jnp.exp(s - lse[..., None])
    o = jnp.einsum('brnhqk,brnkhe->brnqhe', p, vb).reshape(Bsz, dilation, Lp, H, E)[:, :, :L]
    o = o.transpose(0, 2, 1, 3, 4).reshape(Bsz, S, H, E)
    lse = lse.transpose(0, 1, 2, 4, 3).reshape(Bsz, dilation, Lp, H)[:, :, :L]
    lse = lse.transpose(0, 2, 1, 3).reshape(Bsz, S, H)
    return o, lse


def dilated_attention(q, k, v, rel_bias):
    outs, lses = [], []
    for _, dilation in DILATED_PATTERNS:
        o, lse = _dilated_band_attention(q, k, v, rel_bias, dilation)
        outs.append(o)
        lses.append(lse)
    w = jax.nn.softmax(jnp.stack(lses, 0), axis=0)
    return jnp.einsum('pbsh,pbshe->bshe', w, jnp.stack(outs, 0))


def _chunk_gated_scan(q, k, v, log_g):
    Bsz, S, H, K = q.shape
    V = v.shape[-1]
    N = S // CHUNK
    c = lambda t: t.reshape(Bsz, N, CHUNK, H, t.shape[-1])
    q, k, v, log_g = c(q), c(k), c(v), c(log_g)
    b = jnp.cumsum(log_g, axis=2)
    ref = b[:, :, CHUNK // 2 - 1:CHUNK // 2]
    s = jnp.einsum('bnthk,bnshk->bnhts', q * jnp.exp(b - ref), k * jnp.exp(ref - b))
    s = jnp.where(np.tril(np.ones((CHUNK, CHUNK), bool)), s, 0.0)
    o_intra = jnp.einsum('bnhts,bnshv->bnthv', s, v)
    b_last = b[:, :, -1]
    q_dec = q * jnp.exp(b)
    k_dec = k * jnp.exp(b_last[:, :, None] - b)

    def step(state, xs):
        qd, kd, vc, bl = xs
        o = jnp.einsum('bthk,bhkv->bthv', qd, state)
        state = jnp.exp(bl)[..., None] * state + jnp.einsum('bthk,bthv->bhkv', kd, vc)
        return state, o

    xs = (jnp.moveaxis(q_dec, 1, 0), jnp.moveaxis(k_dec, 1, 0), jnp.moveaxis(v, 1, 0), jnp.moveaxis(b_last, 1, 0))
    _, o_inter = lax.scan(step, jnp.zeros((Bsz, H, K, V), jnp.float32), xs)
    return (o_intra + jnp.moveaxis(o_inter, 0, 1)).reshape(Bsz, S, H, V)


def bidirectional_gated_scan(q, k_fwd, g_fwd, k_bwd, g_bwd, v):
    fwd = _chunk_gated_scan(q, k_fwd, v, g_fwd)
    flip = lambda t: jnp.flip(t, axis=1)
    bwd = flip(_chunk_gated_scan(flip(q), flip(k_bwd), flip(v), flip(g_bwd)))
    diag = jnp.sum(q * k_bwd, axis=-1, keepdims=True) * v
    return fwd + bwd - diag


def _head_rmsnorm(o, gain):
    o = o * lax.rsqrt(jnp.mean(o * o, axis=-1, keepdims=True) + NORM_EPS)
    return o.reshape(o.shape[0], o.shape[1], -1) * gain.astype(jnp.float32)


def _layer_norm(z, g, b):
    z32 = z.astype(jnp.float32)
    mu = jnp.mean(z32, axis=-1, keepdims=True)
    var = jnp.mean(jnp.square(z32 - mu), axis=-1, keepdims=True)
    out = (z32 - mu) * lax.rsqrt(var + NORM_EPS) * g.astype(jnp.float32) + b.astype(jnp.float32)
    return out.astype(z.dtype)


def _mixer_ab(h, w_in, gate_up, gate_bias, norm_gain, w_out, rel_bias):
    Bsz, S, _ = h.shape
    proj = jnp.einsum('bsd,df->bsf', h, w_in).astype(jnp.float32)
    qa, ka, va, qb, kb, vb, lr_f, lr_b, gate = _split(
        proj, (A_WIDTH, A_WIDTH, A_WIDTH, B_K_WIDTH, B_K_WIDTH, B_V_WIDTH, B_GATE_RANK, B_GATE_RANK, MIX_WIDTH))
    heads = lambda t, n: t.reshape(Bsz, S, n, -1)
    oa = dilated_attention(heads(qa, A_HEADS), heads(ka, A_HEADS), heads(va, A_HEADS), rel_bias)
    oa = oa.reshape(Bsz, S, A_WIDTH)
    up = gate_up.astype(jnp.float32)
    gb_ = gate_bias.astype(jnp.float32)
    g_f = jax.nn.log_sigmoid(lr_f @ up[0] + gb_[0]) / B_GATE_NORMALIZER
    g_b = jax.nn.log_sigmoid(lr_b @ up[1] + gb_[1]) / B_GATE_NORMALIZER
    qh = heads(qb, B_HEADS) * (B_KEY_DIM ** -0.5)
    kh = heads(kb, B_HEADS)
    ob = bidirectional_gated_scan(qh, kh, heads(g_f, B_HEADS), kh, heads(g_b, B_HEADS), heads(vb, B_HEADS))
    ob = _head_rmsnorm(ob, norm_gain)
    y = jnp.concatenate([oa, ob], axis=-1) * jax.nn.silu(gate)
    return jnp.einsum('bsf,fd->bsd', y.astype(h.dtype), w_out)


def _mixer_c(h, w_in, lower_bounds, layer_idx, norm_gain, w_out):
    Bsz, S, _ = h.shape
    proj = jnp.einsum('bsd,df->bsf', h, w_in).astype(jnp.float32)
    q, f_f, f_b, i, gate = _split(proj, (C_WIDTH,) * 5)
    lb = jax.nn.softmax(lower_bounds.astype(jnp.float32), axis=0)
    lb = (jnp.cumsum(lb, axis=0) - lb[0])[layer_idx]
    heads = lambda t: t.reshape(Bsz, S, C_HEADS, C_HEAD_DIM)

    def forget(z):
        f = lb + (1.0 - lb) * jax.nn.sigmoid(z)
        return heads(1.0 - f), heads(jnp.log(f))

    k_f, g_f = forget(f_f)
    k_b, g_b = forget(f_b)
    qh = heads(jax.nn.silu(q)) * (C_HEAD_DIM ** -0.5)
    o = bidirectional_gated_scan(qh, k_f, g_f, k_b, g_b, heads(i))
    y = _head_rmsnorm(o, norm_gain) * jax.nn.silu(gate)
    return jnp.einsum('bsf,fd->bsd', y.astype(h.dtype), w_out)


def setup_inputs(seed: int = 0) -> dict:
    key = jax.random.key(seed)
    ks = jax.random.split(key, 13)
    f32 = jnp.float32
    nrm = lambda k, shape, s: jax.random.normal(k, shape, f32) * s
    return {
        'x': nrm(ks[0], (BATCH, SEQ, D_MODEL), 1.0),
        'w_in_ab': nrm(ks[1], (N_EVEN, D_MODEL, AB_IN_WIDTH), D_MODEL ** -0.5),
        'gla_gate_up': nrm(ks[2], (N_EVEN, 2, B_GATE_RANK, B_K_WIDTH), B_GATE_RANK ** -0.5),
        'gla_gate_bias': nrm(ks[3], (N_EVEN, 2, B_K_WIDTH), 0.1),
        'gla_norm': 1.0 + nrm(ks[4], (N_EVEN, B_V_WIDTH), 0.02),
        'w_out_ab': nrm(ks[5], (N_EVEN, MIX_WIDTH, D_MODEL), MIX_WIDTH ** -0.5 * DEEPNORM_BETA),
        'w_in_c': nrm(ks[6], (N_ODD, D_MODEL, C_IN_WIDTH), D_MODEL ** -0.5),
        'hgrn_lower_bounds': nrm(ks[7], (DEPTH, C_WIDTH), 0.1),
        'hgrn_norm': 1.0 + nrm(ks[8], (N_ODD, C_WIDTH), 0.02),
        'w_out_c': nrm(ks[9], (N_ODD, C_WIDTH, D_MODEL), C_WIDTH ** -0.5 * DEEPNORM_BETA),
        'rel_bias': nrm(ks[10], (REL_BUCKETS, A_HEADS), 0.1),
        'ln_gain': 1.0 + nrm(ks[11], (DEPTH, D_MODEL), 0.02),
        'ln_bias': nrm(ks[12], (DEPTH, D_MODEL), 0.02),
    }


def reference(x, w_in_ab, gla_gate_up, gla_gate_bias, gla_norm, w_out_ab, w_in_c,
              hgrn_lower_bounds, hgrn_norm, w_out_c, rel_bias, ln_gain, ln_bias):
    for layer in range(DEPTH):
        if layer % 2 == 0:
            e = layer // 2
            y = _mixer_ab(x, w_in_ab[e], gla_gate_up[e], gla_gate_bias[e], gla_norm[e], w_out_ab[e], rel_bias)
        else:
            o = layer // 2
            y = _mixer_c(x, w_in_c[o], hgrn_lower_bounds, layer, hgrn_norm[o], w_out_c[o])
        x = _layer_norm(DEEPNORM_ALPHA * x + y, ln_gain[layer], ln_bias[layer])
    return x
```

```python
import math
from contextlib import ExitStack
import numpy as np
import concourse.bass as bass
import concourse.mybir as mybir

F32 = mybir.dt.float32
BF16 = mybir.dt.bfloat16
AF = mybir.ActivationFunctionType
ALU = mybir.AluOpType
AX = mybir.AxisListType

D_MODEL = 1024
NORM_EPS = 1e-5
DEPTH = 4
ALPHA = (2 * DEPTH) ** 0.25
NEG = -30000.0


class Buf:
    __slots__ = ("name", "w", "r")

    def __init__(self, name):
        self.name = name
        self.w = None
        self.r = []


class KB:
    def __init__(self, nc, n_dma_sems=32, same_engine_sync=True):
        self.nc = nc
        self.eng = {"pe": nc.tensor, "act": nc.scalar, "dve": nc.vector,
                    "pool": nc.gpsimd, "sp": nc.sync}
        self.sems = {}
        self.cnt = {}
        for e in ("pe", "act", "dve", "pool"):
            self.sems[e] = nc.alloc_semaphore(name=f"prog_{e}")
            self.cnt[e] = 0
        self.dma_keys = {"sp": [], "act": [], "pool": []}
        self.dma_rr = {"sp": 0, "act": 0, "pool": 0}
        for q, n in (("sp", 16), ("act", 8), ("pool", 8)):
            for i in range(n):
                k = f"dma_{q}{i}"
                self.sems[k] = nc.alloc_semaphore(name=k)
                self.cnt[k] = 0
                self.dma_keys[q].append(k)
        self.seen = {e: {} for e in self.eng}
        self.same_engine_sync = same_engine_sync
        self.n_inst = {e: 0 for e in self.eng}
        self.n_wait = 0

    def _wait(self, e, tok):
        if tok is None:
            return
        key, val = tok
        if key == e and (e == "pe" or not self.same_engine_sync):
            return
        if self.seen[e].get(key, 0) >= val:
            return
        self.eng[e].wait_ge(self.sems[key], val)
        self.seen[e][key] = val
        self.n_wait += 1

    def _deps(self, e, reads, writes):
        for b in reads:
            self._wait(e, b.w)
        for b in writes:
            self._wait(e, b.w)
            for t in b.r:
                self._wait(e, t)

    def _commit(self, tok, reads, writes):
        for b in reads:
            b.r.append(tok)
            if len(b.r) > 16:
                best = {}
                for k, v in b.r:
                    if best.get(k, 0) < v:
                        best[k] = v
                b.r = list(best.items())
        for b in writes:
            b.w = tok
            b.r = []

    def op(self, e, fn, reads=(), writes=()):
        self._deps(e, reads, writes)
        inst = fn(self.eng[e])
        self.cnt[e] += 1
        inst.then_inc(self.sems[e], 1)
        tok = (e, self.cnt[e])
        self._commit(tok, reads, writes)
        self.n_inst[e] += 1
        return tok

    def dma(self, q, out, in_, reads=(), writes=(), **kw):
        self._deps(q, reads, writes)
        keys = self.dma_keys[q]
        k = keys[self.dma_rr[q]]
        self.dma_rr[q] = (self.dma_rr[q] + 1) % len(keys)
        if self.cnt[k] > 0:
            self._wait(q, (k, self.cnt[k]))
        inst = self.eng[q].dma_start(out=out, in_=in_, **kw)
        self.cnt[k] += 16
        inst.then_inc(self.sems[k], 16)
        tok = (k, self.cnt[k])
        self._commit(tok, reads, writes)
        self.n_inst[q] += 1
        return tok

    def full_barrier(self):
        for e in self.eng:
            for k, v in self.cnt.items():
                if v > 0:
                    self._wait(e, (k, v))

    def finish(self):
        for k, v in self.cnt.items():
            if v > 0:
                self._wait("sp", (k, v))


def scan_consts(c):
    s = np.arange(128)[:, None]
    t = np.arange(128)[None, :]
    same = (s // 32) == (t // 32)
    sl, tl = s % 32, t % 32
    out = {}
    out["M1f"] = c * same * ((sl <= tl).astype(np.float32) - (sl <= 15).astype(np.float32))
    out["M3f"] = c * same * (sl <= tl).astype(np.float32)
    out["M4f"] = c * same * (sl > tl).astype(np.float32)
    out["Kf"] = (same & (sl <= tl)).astype(np.float32)
    out["M1b"] = c * same * ((sl >= tl).astype(np.float32) - (sl >= 16).astype(np.float32))
    out["M3b"] = c * same * (sl >= tl).astype(np.float32)
    out["M4b"] = c * same * (sl < tl).astype(np.float32)
    out["Kb"] = (same & (sl > tl)).astype(np.float32)
    ones4 = np.zeros((128, 4), np.float32)
    ones4[np.arange(128), np.arange(128) // 32] = c
    out["O4"] = ones4
    return {k: np.ascontiguousarray(v, dtype=np.float32) for k, v in out.items()}


def pack_consts():
    parts = [("ident", np.eye(128, dtype=np.float32))]
    sel = np.zeros((128, 64), np.float32)
    sel[64 + np.arange(64), np.arange(64)] = 1.0
    parts.append(("sel", sel))
    sc = scan_consts(1.0)
    parts.append(("Kf", np.tile(sc["Kf"], (1, 4))))
    parts.append(("Kb", np.tile(sc["Kb"], (1, 4))))
    for pre, c in (("c_", 1.0), ("g_", -1.0 / 16.0)):
        sc = scan_consts(c)
        for k in ("M1f", "M3f", "M4f", "M1b", "M3b", "M4b", "O4"):
            parts.append((pre + k, sc[k]))
    offs = {}
    o = 0
    for k, v in parts:
        offs[k] = (o, v.shape[1])
        o += v.shape[1]
    arr = np.concatenate([v for _, v in parts], axis=1)
    return np.ascontiguousarray(arr, dtype=np.float32), offs


class Gen:
    def __init__(self, T):
        self.T = T
        self.NT = T // 128
        self.nc = bass.Bass("TRN2", target_bir_lowering=False)
        self.kb = KB(self.nc)
        nc = self.nc
        self.bank = [nc.alloc_psum_tensor(f"bank{i}", [128, 512], F32) for i in range(8)]
        self.bbank = [Buf(f"bank{i}") for i in range(8)]
        self.es = None
        self._uid = 0

    def sb(self, name, shape, dtype):
        self._uid += 1
        t = self.es.enter_context(self.nc.sbuf_tensor(f"{name}_{self._uid}", list(shape), dtype))
        return t, Buf(name)

    def sb_perm(self, name, shape, dtype):
        t = self.nc.alloc_sbuf_tensor(name, list(shape), dtype)
        return t, Buf(name)

    def ring(self, name, shape, dtype, n):
        items = [self.sb(f"{name}{i}", shape, dtype) for i in range(n)]
        return Ring(items)

    def dram(self, name, shape, dtype, kind="Internal"):
        t = self.nc.dram_tensor(name, list(shape), dtype, kind=kind)
        return t.ap(), Buf(name)

    def begin_pass(self):
        self.es = ExitStack()

    def end_pass(self):
        self.kb.full_barrier()
        self.es.close()
        self.es = None

    def pe(self, fn, r=(), w=()):
        return self.kb.op("pe", fn, r, w)

    def act(self, fn, r=(), w=()):
        return self.kb.op("act", fn, r, w)

    def dve(self, fn, r=(), w=()):
        return self.kb.op("dve", fn, r, w)

    def pool(self, fn, r=(), w=()):
        return self.kb.op("pool", fn, r, w)

    def dma(self, out, in_, r=(), w=(), q="sp"):
        return self.kb.dma(q, out, in_, r, w)


class Ring:
    def __init__(self, items):
        self.items = items
        self.i = 0

    def next(self):
        it = self.items[self.i]
        self.i = (self.i + 1) % len(self.items)
        return it


def load_consts(g, cst_ap, offs):
    nc = g.nc
    ncols = cst_ap.shape[1]
    C, bC = g.sb_perm("cst_sb", [128, ncols], F32)
    g.dma(C[:], cst_ap, w=[bC])
    idb, bidb = g.sb_perm("identb", [128, 128], BF16)
    o, n = offs["ident"]
    g.dve(lambda e: e.tensor_copy(idb[:], C[:, o:o + n]), r=[bC], w=[bidb])
    zr, bzr = g.sb_perm("zrow", [1, 512], BF16)
    g.dve(lambda e: e.memset(zr[:], 0.0), w=[bzr])
    Kfb, bKf = g.sb_perm("Kfb", [128, 512], BF16)
    Kbb, bKb = g.sb_perm("Kbb", [128, 512], BF16)
    o, n = offs["Kf"]
    g.dve(lambda e: e.tensor_copy(Kfb[:], C[:, o:o + n]), r=[bC], w=[bKf])
    o, n = offs["Kb"]
    g.dve(lambda e: e.tensor_copy(Kbb[:], C[:, o:o + n]), r=[bC], w=[bKb])
    g.C, g.bC, g.offs = C, bC, offs
    g.idb, g.bidb = idb, bidb
    g.zr, g.bzr = zr, bzr
    g.maskb = {"f": (Kfb, bKf), "b": (Kbb, bKb)}

    def cview(name):
        o, n = offs[name]
        return C[:, o:o + n]
    g.cview = cview


def cast_weight(g, dst_ap, bdst, src_ap, rows, cols):
    step = 128
    for r0 in range(0, rows, step):
        g.dma(dst_ap[r0:r0 + step, :], src_ap[r0:r0 + step, :], w=[bdst], q="pool")


def load_x_transposed(g, x_ap, bx, mt, xT, bxT, xs_ring, tp_banks):
    ident = g.cview("ident")
    for j in range(4):
        xs, bxs = xs_ring.next()
        r0 = (mt * 4 + j) * 128
        g.dma(xs[:], x_ap[r0:r0 + 128, :], r=[bx], w=[bxs])
        for half in range(2):
            bi = tp_banks[(j * 2 + half) % len(tp_banks)]
            pt, bpt = g.bank[bi], g.bbank[bi]
            for cc in range(4):
                c = half * 4 + cc
                g.pe(lambda e: e.transpose(pt[:, cc * 128:(cc + 1) * 128], xs[:, c * 128:(c + 1) * 128], ident),
                     r=[bxs, g.bC], w=[bpt])
            eng = g.act if half == 0 else g.dve
            if half == 0:
                g.act(lambda e: e.activation(xT[:, half * 4:half * 4 + 4, j * 128:(j + 1) * 128],
                                             pt[:].rearrange("p (c t) -> p c t", c=4), AF.Copy),
                      r=[bpt], w=[bxT])
            else:
                g.dve(lambda e: e.tensor_copy(xT[:, half * 4:half * 4 + 4, j * 128:(j + 1) * 128],
                                              pt[:].rearrange("p (c t) -> p c t", c=4)),
                      r=[bpt], w=[bxT])


def compute_lb(g, lb_raw_ap, layer_idx, lbB, blbB, omlbB, bomlbB):
    with ExitStack() as es:
        def tmp(name, shape):
            g._uid += 1
            t = es.enter_context(g.nc.sbuf_tensor(f"{name}_{g._uid}", shape, F32))
            return t, Buf(name)
        raw, braw = tmp("lbraw", [128, 4, 1536])
        g.dma(raw[:].rearrange("p a f -> p (a f)"),
              lb_raw_ap.rearrange("a f -> (a f)").partition_broadcast(128), w=[braw])
        g.act(lambda e: e.activation(raw[:], raw[:], AF.Exp), r=[braw], w=[braw])
        den, bden = tmp("lbden", [128, 1536])
        num, bnum = tmp("lbnum", [128, 1536])
        g.dve(lambda e: e.tensor_tensor(den[:], raw[:, 0, :], raw[:, 1, :], ALU.add), r=[braw], w=[bden])
        g.dve(lambda e: e.tensor_tensor(den[:], den[:], raw[:, 2, :], ALU.add), r=[braw, bden], w=[bden])
        g.dve(lambda e: e.tensor_tensor(den[:], den[:], raw[:, 3, :], ALU.add), r=[braw, bden], w=[bden])
        g.dve(lambda e: e.reciprocal(den[:], den[:]), r=[bden], w=[bden])
        g.dve(lambda e: e.tensor_copy(num[:], raw[:, 1, :]), r=[braw], w=[bnum])
        for a in range(2, layer_idx + 1):
            g.dve(lambda e: e.tensor_tensor(num[:], num[:], raw[:, a, :], ALU.add), r=[braw, bnum], w=[bnum])
        g.dve(lambda e: e.tensor_tensor(lbB[:], num[:], den[:], ALU.mult), r=[bnum, bden], w=[blbB])
        g.dve(lambda e: e.tensor_scalar(omlbB[:], lbB[:], -1.0, 1.0, ALU.mult, ALU.add), r=[blbB], w=[bomlbB])
        g.kb.full_barrier()


def pass_P_C(g, L):
    T = g.T
    NM = T // 512
    g.begin_pass()
    x_ap, bx = L["x_in"]
    w_bf, bw = L["w_in_bf"]
    lbB, blbB = g.sb("lbB", [128, 1536], F32)
    omlbB, bomlbB = g.sb("omlbB", [128, 1536], F32)
    compute_lb(g, L["lb_raw"], L["layer_idx"], lbB, blbB, omlbB, bomlbB)
    xs_ring = g.ring("xs", [128, 1024], F32, 2)
    xT_ring = g.ring("xT", [128, 8, 512], BF16, 2)
    wb_ring = g.ring("wb", [128, 8, 512], BF16, 3)
    qs_ring = g.ring("qs", [128, 512], BF16, 3)
    sg_ring = g.ring("sg", [128, 512], F32, 4)
    f2_ring = g.ring("f2", [128, 512], F32, 4)
    lgo_ring = g.ring("lgo", [128, 512], F32, 3)
    bo_ring = g.ring("bo", [128, 512], BF16, 6)
    acc_banks = [0, 1, 2, 3, 4, 5]
    tp_banks = [6, 7]
    acc_i = [0]

    def next_bank():
        bi = acc_banks[acc_i[0] % len(acc_banks)]
        acc_i[0] += 1
        return g.bank[bi], g.bbank[bi]

    w_v = w_bf.rearrange("(c p) f -> p c f", p=128)
    for mt in range(NM):
        xT, bxT = xT_ring.next()
        load_x_transposed(g, x_ap, bx, mt, xT, bxT, xs_ring, tp_banks)
        for blk in range(15):
            wb, bwb = wb_ring.next()
            g.dma(wb[:], w_v[:, :, blk * 512:(blk + 1) * 512], r=[bw], w=[bwb])
            if blk < 3:
                for fg in range(4):
                    pp, bpp = next_bank()
                    for c in range(8):
                        g.pe(lambda e: e.matmul(pp[:], wb[:, c, fg * 128:(fg + 1) * 128], xT[:, c, :],
                                                start=(c == 0), stop=(c == 7)), r=[bwb, bxT], w=[bpp])
                    qs, bqs = qs_ring.next()
                    g.act(lambda e: e.activation(qs[:], pp[:], AF.Silu), r=[bpp], w=[bqs])
                    h = blk * 4 + fg
                    g.dma(L["qT_s"][0][mt * 4:(mt + 1) * 4, :, h, :].rearrange("j k t -> k j t"),
                          qs[:].rearrange("k (j t) -> k j t", j=4), r=[bqs], q="act")
                continue
            kind = (blk - 3) // 3
            c0 = ((blk - 3) % 3) * 512
            pps = []
            for j in range(4):
                pp, bpp = next_bank()
                for c in range(8):
                    g.pe(lambda e: e.matmul(pp[:], xT[:, c, j * 128:(j + 1) * 128], wb[:, c, :],
                                            start=(c == 0), stop=(c == 7)), r=[bwb, bxT], w=[bpp])
                r0 = (mt * 4 + j) * 128
                if kind in (0, 1):
                    sg, bsg = sg_ring.next()
                    f2, bf2 = f2_ring.next()
                    g.act(lambda e: e.activation(sg[:], pp[:], AF.Sigmoid), r=[bpp], w=[bsg])
                    g.dve(lambda e: e.tensor_tensor(sg[:], sg[:], omlbB[:, c0:c0 + 512], ALU.mult),
                          r=[bsg, bomlbB], w=[bsg])
                    g.pool(lambda e: e.tensor_tensor(f2[:], sg[:], lbB[:, c0:c0 + 512], ALU.add),
                           r=[bsg, blbB], w=[bf2])
                    pps.append((f2, bf2, r0))
                elif kind == 2:
                    bo, bbo = bo_ring.next()
                    g.dve(lambda e: e.tensor_copy(bo[:], pp[:]), r=[bpp], w=[bbo])
                    g.dma(L["v_s"][0][r0:r0 + 128, c0:c0 + 512], bo[:], r=[bbo], q="sp")
                else:
                    bo, bbo = bo_ring.next()
                    g.act(lambda e: e.activation(bo[:], pp[:], AF.Silu), r=[bpp], w=[bbo])
                    g.dma(L["gs_s"][0][r0:r0 + 128, c0:c0 + 512], bo[:], r=[bbo], q="act")
            if kind in (0, 1):
                lg_dst = L["lgf_s"][0] if kind == 0 else L["lgb_s"][0]
                k_dst = L["kf_s"][0] if kind == 0 else L["kb_s"][0]
                for (f2, bf2, r0) in pps:
                    lgo, blgo = lgo_ring.next()
                    bo, bbo = bo_ring.next()
                    g.act(lambda e: e.activation(lgo[:], f2[:], AF.Ln), r=[bf2], w=[blgo])
                    g.dma(lg_dst[r0:r0 + 128, c0:c0 + 512], lgo[:], r=[blgo], q="act")
                    g.pool(lambda e: e.tensor_scalar(bo[:], f2[:], -1.0, 1.0, ALU.mult, ALU.add),
                           r=[bf2], w=[bbo])
                    g.dma(k_dst[r0:r0 + 128, c0:c0 + 512], bo[:], r=[bbo], q="pool")
    g.end_pass()


def pass_scan(g, S, dirn, state_only=False):
    H, V, G = S["H"], S["V"], S["G"]
    NG = H // G
    GK = G * 128
    GV = G * V
    NT = g.NT
    g.begin_pass()
    cpre = S["cpre"]
    M1 = g.cview(cpre + "M1" + dirn)
    M3 = g.cview(cpre + "M3" + dirn)
    M4 = g.cview(cpre + "M4" + dirn)
    O4 = g.cview(cpre + "O4")
    maskb, bmask = g.maskb[dirn]
    k_s = S["k_s"][dirn][0]
    lg_s = S["lg_s"][dirn][0]
    v_s = S["v_s"][0]
    of_s = S["of_s"][0]
    qT_s = S["qT_s"][0]
    lnscale = S["lnscale"]

    nld = 2
    q_ring = g.ring("qt", [128, H, 128], BF16, nld) if not state_only else None
    k_ring = g.ring("kt", [128, H * 128], BF16, nld)
    l_ring = g.ring("lt", [128, H * 128], F32, nld)
    v_ring = g.ring("vt", [128, H * V], BF16, nld)
    of_ring = g.ring("oft", [128, H * V], F32, nld) if (dirn == "b" and not state_only) else None
    ot_ring = g.ring("ot", [128, H * V], F32, 2) if not state_only else None
    e_ring = g.ring("e", [128, 512], F32, 8)
    p_ring = g.ring("p", [128, 512], BF16, 6)
    pl_ring = g.ring("pl", [128, 512], BF16, 4 * NG)
    dec_ring = g.ring("dec", [128, G * 4], F32, 2 * NG)
    lnb, blnb = g.sb("lnb", [128, 1], F32)
    g.dve(lambda e: e.memset(lnb[:], lnscale), w=[blnb])
    St = []
    for gi in range(NG):
        s_, bs_ = g.sb(f"S{gi}", [128, GV], F32)
        sb_, bsb_ = g.sb(f"Sb{gi}", [128, GV], BF16)
        if S.get("init") is not None:
            src = S["init"][dirn][0]
            g.dma(s_[:], src[:, gi * GV:(gi + 1) * GV], r=[S["init"][dirn][1]], w=[bs_])
        else:
            g.dve(lambda e: e.memset(s_[:], 0.0), w=[bs_])
        g.pool(lambda e: e.tensor_copy(sb_[:], s_[:]), r=[bs_], w=[bsb_])
        St.append((s_, bs_, sb_, bsb_))

    gate_banks = [0, 1, 2]
    gb_i = [0]

    def gbank():
        bi = gate_banks[gb_i[0] % 3]
        gb_i[0] += 1
        return g.bank[bi], g.bbank[bi]

    po_banks = [3, 4, 5][:NG]
    ds_banks = [6, 7]
    ds_i = [0]

    tiles = list(range(NT)) if dirn == "f" else list(range(NT - 1, -1, -1))
    chunks = [0, 1, 2, 3] if dirn == "f" else [3, 2, 1, 0]
    for ti in tiles:
        r0 = ti * 128
        kt, bkt = k_ring.next()
        g.dma(kt[:], k_s[r0:r0 + 128, :], w=[bkt])
        lt, blt = l_ring.next()
        g.dma(lt[:], lg_s[r0:r0 + 128, :], w=[blt])
        vt, bvt = v_ring.next()
        g.dma(vt[:], v_s[r0:r0 + 128, :], w=[bvt])
        if not state_only:
            qt, bqt = q_ring.next()
            g.dma(qt[:], qT_s[ti], w=[bqt])
            if dirn == "b":
                oft, boft = of_ring.next()
                g.dma(oft[:], of_s[r0:r0 + 128, :], w=[boft])
        per_g = []
        for gi in range(NG):
            h0 = gi * G
            e4, be4 = e_ring.next()
            pA4, bA4 = gbank()
            g.pe(lambda e: e.matmul(pA4[:, 0:GK], M4, lt[:, h0 * 128:h0 * 128 + GK], start=True, stop=True),
                 r=[g.bC, blt], w=[bA4])
            g.act(lambda e: e.activation(e4[:, 0:GK], pA4[:, 0:GK], AF.Exp), r=[bA4], w=[be4])
            kdec, bkdec = pl_ring.next()
            g.pool(lambda e: e.tensor_tensor(kdec[:, 0:GK], kt[:, h0 * 128:h0 * 128 + GK], e4[:, 0:GK], ALU.mult),
                   r=[bkt, be4], w=[bkdec])
            pK, bK = gbank()
            pKb = pK[:, 0:256].bitcast(BF16)
            pB = pK[:, 256:256 + G * 4]
            for hh in range(G):
                h = h0 + hh
                g.pe(lambda e: e.matmul(pB[:, hh * 4:(hh + 1) * 4], lt[:, h * 128:(h + 1) * 128], O4,
                                        start=True, stop=True), r=[blt, g.bC], w=[bK])
            dec, bdec = dec_ring.next()
            if state_only:
                g.act(lambda e: e.activation(dec[:], pB, AF.Exp), r=[bK], w=[bdec])
                per_g.append(dict(kdec=(kdec, bkdec), dec=(dec, bdec)))
                continue
            for hh in range(G):
                h = h0 + hh
                g.pe(lambda e: e.transpose(pKb[:, hh * 128:(hh + 1) * 128], kt[:, h * 128:(h + 1) * 128], g.idb[:]),
                     r=[bkt, g.bidb], w=[bK])
            g.act(lambda e: e.activation(dec[:], pB, AF.Exp), r=[bK], w=[bdec])
            pA1, bA1 = gbank()
            for hh in range(G):
                h = h0 + hh
                g.pe(lambda e: e.matmul(pA1[:, hh * 128:(hh + 1) * 128], lt[:, h * 128:(h + 1) * 128], M1,
                                        start=True, stop=True), r=[blt, g.bC], w=[bA1])
            e1, be1 = e_ring.next()
            e1n, be1n = e_ring.next()
            g.act(lambda e: e.activation(e1[:, 0:GK], pA1[:, 0:GK], AF.Exp, bias=lnb[:]), r=[bA1, blnb], w=[be1])
            g.act(lambda e: e.activation(e1n[:, 0:GK], pA1[:, 0:GK], AF.Exp, scale=-1.0), r=[bA1], w=[be1n])
            pA3, bA3 = gbank()
            for hh in range(G):
                h = h0 + hh
                g.pe(lambda e: e.matmul(pA3[:, hh * 128:(hh + 1) * 128], lt[:, h * 128:(h + 1) * 128], M3,
                                        start=True, stop=True), r=[blt, g.bC], w=[bA3])
            e3, be3 = e_ring.next()
            g.act(lambda e: e.activation(e3[:, 0:GK], pA3[:, 0:GK], AF.Exp, bias=lnb[:]), r=[bA3, blnb], w=[be3])
            qv = qt[:, h0:h0 + G, :]
            qd, bqd = p_ring.next()
            g.pool(lambda e: e.tensor_tensor(qd[:, 0:GK].rearrange("p (h t) -> p h t", h=G), qv,
                                             e1[:, 0:GK].rearrange("p (h t) -> p h t", h=G), ALU.mult),
                   r=[bqt, be1], w=[bqd])
            qdec, bqdec = pl_ring.next()
            g.pool(lambda e: e.tensor_tensor(qdec[:, 0:GK].rearrange("p (h t) -> p h t", h=G), qv,
                                             e3[:, 0:GK].rearrange("p (h t) -> p h t", h=G), ALU.mult),
                   r=[bqt, be3], w=[bqdec])
            kdT, bkdT = p_ring.next()
            g.dve(lambda e: e.tensor_tensor(kdT[:, 0:GK], pKb[:, 0:GK], e1n[:, 0:GK], ALU.mult),
                  r=[bK, be1n], w=[bkdT])
            pS, bS = gbank()
            for hh in range(G):
                g.pe(lambda e: e.matmul(pS[:, hh * 128:(hh + 1) * 128], kdT[:, hh * 128:(hh + 1) * 128],
                                        qd[:, hh * 128:(hh + 1) * 128], start=True, stop=True),
                     r=[bkdT, bqd], w=[bS])
            sTm, bsTm = p_ring.next()
            g.dve(lambda e: e.tensor_tensor(sTm[:, 0:GK], pS[:, 0:GK], maskb[:, 0:GK], ALU.mult),
                  r=[bS, bmask], w=[bsTm])
            pO, bO = g.bank[po_banks[gi]], g.bbank[po_banks[gi]]
            g.pe(lambda e: e.matmul(pO[:, :], g.zr[0:1, 0:128], g.zr[0:1, 0:512], start=True, stop=False),
                 r=[g.bzr], w=[bO])
            for hh in range(G):
                h = h0 + hh
                g.pe(lambda e: e.matmul(pO[:, hh * V:(hh + 1) * V], sTm[:, hh * 128:(hh + 1) * 128],
                                        vt[:, h * V:(h + 1) * V], start=False, stop=False),
                     r=[bsTm, bvt], w=[bO])
            per_g.append(dict(kdec=(kdec, bkdec), dec=(dec, bdec), qdec=(qdec, bqdec), pO=(pO, bO)))
        for ci, c in enumerate(chunks):
            for gi in range(NG):
                h0 = gi * G
                pg = per_g[gi]
                s_, bs_, sb_, bsb_ = St[gi]
                kdec, bkdec = pg["kdec"]
                dec, bdec = pg["dec"]
                if not state_only:
                    qdec, bqdec = pg["qdec"]
                    pO, bO = pg["pO"]
                    for hh in range(G):
                        last = False
                        g.pe(lambda e: e.matmul(pO[32 * c:32 * c + 32, hh * V:(hh + 1) * V],
                                                qdec[:, hh * 128 + 32 * c:hh * 128 + 32 * c + 32],
                                                sb_[:, hh * V:(hh + 1) * V], start=False, stop=last,
                                                tile_position=(0, 32 * c)),
                             r=[bqdec, bsb_], w=[bO])
                    if ci == 3:
                        g.pe(lambda e: e.matmul(pO[:, :], g.zr[0:1, 0:128], g.zr[0:1, 0:512], start=False, stop=True),
                             r=[g.bzr], w=[bO])
                bi = ds_banks[ds_i[0] % 2]
                ds_i[0] += 1
                pD, bD = g.bank[bi], g.bbank[bi]
                for hh in range(G):
                    h = h0 + hh
                    g.pe(lambda e: e.matmul(pD[:, hh * V:(hh + 1) * V],
                                            kdec[32 * c:32 * c + 32, hh * 128:(hh + 1) * 128],
                                            vt[32 * c:32 * c + 32, h * V:(h + 1) * V], start=True, stop=True,
                                            tile_position=(32 * c, 0)),
                         r=[bkdec, bvt], w=[bD])
                for hh in range(G):
                    g.dve(lambda e: e.scalar_tensor_tensor(s_[:, hh * V:(hh + 1) * V], s_[:, hh * V:(hh + 1) * V],
                                                           dec[:, hh * 4 + c:hh * 4 + c + 1],
                                                           pD[:, hh * V:(hh + 1) * V], ALU.mult, ALU.add),
                          r=[bs_, bdec, bD], w=[bs_])
                if not state_only:
                    g.pool(lambda e: e.tensor_copy(sb_[:], s_[:]), r=[bs_], w=[bsb_])
        if not state_only:
            ot, bot = ot_ring.next()
            for gi in range(NG):
                pO, bO = per_g[gi]["pO"]
                if dirn == "f":
                    g.act(lambda e: e.activation(ot[:, gi * GV:(gi + 1) * GV], pO[:, :], AF.Copy), r=[bO], w=[bot])
                else:
                    g.dve(lambda e: e.tensor_tensor(ot[:, gi * GV:(gi + 1) * GV], pO[:, :],
                                                    oft[:, gi * GV:(gi + 1) * GV], ALU.add),
                          r=[bO, boft], w=[bot])
            g.dma(of_s[r0:r0 + 128, :], ot[:], r=[bot], q="sp")
    if S.get("fin") is not None:
        dst = S["fin"][dirn][0]
        for gi in range(NG):
            s_, bs_, _, _ = St[gi]
            g.dma(dst[:, gi * GV:(gi + 1) * GV], s_[:], r=[bs_], q="sp")
    g.end_pass()


def pass_epilogue(g, E):
    H, V = E["H"], E["V"]
    HV = H * V
    NT = g.NT
    NB = HV // 128
    has_a = E.get("yaT_s") is not None
    NA = 4 if has_a else 0
    g.begin_pass()
    wout, bwout = g.sb("wout", [128, 12, 1024], BF16)
    g.dma(wout[:], E["wout_bf"][0].rearrange("(c p) d -> p c d", p=128), r=[E["wout_bf"][1]], w=[bwout])
    gainB, bgainB = g.sb("gainB", [128, HV], F32)
    g.dma(gainB[:], E["gain_ap"].partition_broadcast(128), w=[bgainB])
    lnG, blnG = g.sb("lnG", [128, 1024], F32)
    g.dma(lnG[:], E["lng_ap"].partition_broadcast(128), w=[blnG])
    lnBt, blnB = g.sb("lnB", [128, 1024], F32)
    g.dma(lnBt[:], E["lnb_ap"].partition_broadcast(128), w=[blnB])
    epsb, bepsb = g.sb("epsb", [128, 1], F32)
    g.dve(lambda e: e.memset(epsb[:], NORM_EPS), w=[bepsb])
    o_ring = g.ring("eo", [128, HV], F32, 2)
    gs_ring = g.ring("egs", [128, HV], BF16, 2)
    x_ring = g.ring("ex", [128, 1024], F32, 2)
    gg_ring = g.ring("egg", [128, HV], F32, 2)
    sq_ring = g.ring("esq", [128, HV], F32, 1)
    y_ring = g.ring("ey", [128, HV], BF16, 2)
    yT_ring = g.ring("eyT", [128, 12, 128], BF16, 2)
    z_ring = g.ring("ez", [128, 1024], F32, 2)
    xo_ring = g.ring("exo", [128, 1024], F32, 2)
    st_ring = g.ring("est", [128, 32], F32, 2)
    o_s = E["o_s"][0]
    gs_s = E["gs_s"][0]
    x_in = E["x_in"][0]
    x_out = E["x_out"][0]
    tp_banks = [0, 1, 2]
    tp_i = 0
    po_banks = [3, 4, 5, 6]
    po_i = 0
    for ti in range(NT):
        r0 = ti * 128
        ot, bot = o_ring.next()
        g.dma(ot[:], o_s[r0:r0 + 128, :], w=[bot])
        gst, bgst = gs_ring.next()
        g.dma(gst[:], gs_s[r0:r0 + 128, :], w=[bgst])
        xt, bxt = x_ring.next()
        g.dma(xt[:], x_in[r0:r0 + 128, :], w=[bxt])
        yT, byT = yT_ring.next()
        if has_a:
            g.dma(yT[:, 0:4, :], E["yaT_s"][0].rearrange("(c p) t -> p c t", p=128)[:, :, r0:r0 + 128], w=[byT])
        gg, bgg = gg_ring.next()
        g.pool(lambda e: e.tensor_tensor(gg[:], gst[:], gainB[:], ALU.mult), r=[bgst, bgainB], w=[bgg])
        sq, bsq = sq_ring.next()
        g.act(lambda e: e.activation(sq[:], ot[:], AF.Square), r=[bot], w=[bsq])
        st, bst = st_ring.next()
        g.dve(lambda e: e.tensor_reduce(st[:, 0:H], sq[:].rearrange("p (h v) -> p h v", h=H), AX.X, ALU.add),
              r=[bsq], w=[bst])
        g.act(lambda e: e.activation(st[:, 0:H], st[:, 0:H], AF.Sqrt, bias=epsb[:], scale=1.0 / V),
              r=[bst, bepsb], w=[bst])
        g.dve(lambda e: e.reciprocal(st[:, 0:H], st[:, 0:H]), r=[bst], w=[bst])
        g.dve(lambda e: e.tensor_tensor(sq[:].rearrange("p (h v) -> p h v", h=H),
                                        ot[:].rearrange("p (h v) -> p h v", h=H),
                                        st[:, 0:H].unsqueeze(2).to_broadcast([128, H, V]), ALU.mult),
              r=[bot, bst], w=[bsq])
        y, by = y_ring.next()
        g.pool(lambda e: e.tensor_tensor(y[:], sq[:], gg[:], ALU.mult), r=[bsq, bgg], w=[by])
        for b0 in range(0, NB, 4):
            bi = tp_banks[tp_i % 3]
            tp_i += 1
            pT, bT = g.bank[bi], g.bbank[bi]
            pTb = pT[:, 0:256].bitcast(BF16)
            for bb in range(4):
                b = b0 + bb
                g.pe(lambda e: e.transpose(pTb[:, bb * 128:(bb + 1) * 128], y[:, b * 128:(b + 1) * 128], g.idb[:]),
                     r=[by, g.bidb], w=[bT])
            g.act(lambda e: e.activation(yT[:, NA + b0:NA + b0 + 4, :],
                                         pTb[:, 0:512].rearrange("p (c t) -> p c t", c=4), AF.Copy),
                  r=[bT], w=[byT])
        z, bz = z_ring.next()
        for half in range(2):
            bi = po_banks[po_i % 4]
            po_i += 1
            pP, bP = g.bank[bi], g.bbank[bi]
            for fc in range(12):
                g.pe(lambda e: e.matmul(pP[:], yT[:, fc, :], wout[:, fc, half * 512:(half + 1) * 512],
                                        start=(fc == 0), stop=(fc == 11)), r=[byT, bwout], w=[bP])
            g.dve(lambda e: e.scalar_tensor_tensor(z[:, half * 512:(half + 1) * 512],
                                                   xt[:, half * 512:(half + 1) * 512], float(ALPHA), pP[:],
                                                   ALU.mult, ALU.add), r=[bxt, bP], w=[bz])
        for half in range(2):
            g.dve(lambda e: e.bn_stats(st[:, 12 + half * 6:12 + half * 6 + 6], z[:, half * 512:(half + 1) * 512]),
                  r=[bz], w=[bst])
        g.dve(lambda e: e.bn_aggr(st[:, 24:26], st[:, 12:24]), r=[bst], w=[bst])
        g.act(lambda e: e.activation(st[:, 25:26], st[:, 25:26], AF.Sqrt, bias=epsb[:], scale=1.0),
              r=[bst, bepsb], w=[bst])
        g.dve(lambda e: e.reciprocal(st[:, 25:26], st[:, 25:26]), r=[bst], w=[bst])
        g.dve(lambda e: e.tensor_scalar(z[:], z[:], st[:, 24:25], st[:, 25:26], ALU.subtract, ALU.mult),
              r=[bz, bst], w=[bz])
        xo, bxo = xo_ring.next()
        g.pool(lambda e: e.tensor_tensor(xo[:], z[:], lnG[:], ALU.mult), r=[bz, blnG], w=[bxo])
        g.pool(lambda e: e.tensor_tensor(xo[:], xo[:], lnBt[:], ALU.add), r=[bxo, blnB], w=[bxo])
        g.dma(x_out[r0:r0 + 128, :], xo[:], r=[bxo], q="pool")
    g.end_pass()


def layer_C(g, name, x_in, x_out, w_in_ap, w_out_ap, lb_raw_ap, layer_idx, gain_ap, lng_ap, lnb_ap, sc=None):
    T = g.T
    if sc is None:
        sc = {}
    def scr(key, shape, dtype):
        if key not in sc:
            sc[key] = g.dram(f"{key}", shape, dtype)
        return sc[key]
    w_in_bf = scr("c_w_in_bf", [1024, 7680], BF16)
    wout_bf = scr("wout_bf", [1536, 1024], BF16)
    cast_weight(g, w_in_bf[0], w_in_bf[1], w_in_ap, 1024, 7680)
    cast_weight(g, wout_bf[0], wout_bf[1], w_out_ap, 1536, 1024)
    g.kb.full_barrier()
    L = dict(x_in=x_in, w_in_bf=w_in_bf, lb_raw=lb_raw_ap, layer_idx=layer_idx,
             qT_s=scr("c_qT", [T // 128, 128, 12, 128], BF16),
             kf_s=scr("c_kf", [T, 1536], BF16), kb_s=scr("c_kb", [T, 1536], BF16),
             lgf_s=scr("c_lgf", [T, 1536], F32), lgb_s=scr("c_lgb", [T, 1536], F32),
             v_s=scr("c_v", [T, 1536], BF16), gs_s=scr("c_gs", [T, 1536], BF16),
             of_s=scr("c_of", [T, 1536], F32))
    pass_P_C(g, L)
    S = dict(H=12, V=128, G=4, cpre="c_", lnscale=math.log(128 ** -0.5), qT_s=L["qT_s"],
             k_s={"f": L["kf_s"], "b": L["kb_s"]}, lg_s={"f": L["lgf_s"], "b": L["lgb_s"]},
             v_s=L["v_s"], of_s=L["of_s"])
    pass_scan(g, S, "f")
    pass_scan(g, S, "b")
    E = dict(H=12, V=128, o_s=L["of_s"], gs_s=L["gs_s"], gain_ap=gain_ap, wout_bf=wout_bf,
             x_in=x_in, x_out=x_out, lng_ap=lng_ap, lnb_ap=lnb_ap)
    pass_epilogue(g, E)


HALO = 1024


def pass_P_AB(g, L):
    T = g.T
    NM = T // 512
    g.begin_pass()
    x_ap, bx = L["x_in"]
    w_bf, bw = L["w_in_bf"]
    xs_ring = g.ring("xs", [128, 1024], F32, 2)
    xT_ring = g.ring("xT", [128, 8, 512], BF16, 2)
    wb_ring = g.ring("wb", [128, 8, 512], BF16, 3)
    fo_ring = g.ring("fo", [128, 512], BF16, 4)
    to_ring = g.ring("to", [128, 512], BF16, 6)
    lo_ring = g.ring("lo", [128, 512], F32, 4)
    ex_ring = g.ring("ex", [128, 512], F32, 3)
    lra = []
    for dd in range(2):
        t_, b_ = g.sb(f"lra{dd}", [17, 512], F32)
        lra.append((t_, b_))
    upa = []
    for dd in range(2):
        t_, b_ = g.sb(f"upa{dd}", [17, 512], F32)
        g.dma(t_[0:16, :], L["gate_up"][dd], w=[b_])
        g.dma(t_[16:17, :], L["gate_bias"][dd:dd + 1, :], w=[b_])
        upa.append((t_, b_))
    oneb, boneb = g.sb("oneb", [128, 1], F32)
    g.dve(lambda e: e.memset(oneb[:], 1.0), w=[boneb])
    acc_banks = [0, 1, 2, 3, 4, 5]
    tp_banks = [6, 7]
    acc_i = [0]

    def next_bank():
        bi = acc_banks[acc_i[0] % len(acc_banks)]
        acc_i[0] += 1
        return g.bank[bi], g.bbank[bi]

    w_v = w_bf.rearrange("(c p) f -> p c f", p=128)
    blocks = [("qa", 0, 512, "fm"), ("ka", 512, 512, "fm"), ("va", 1024, 512, "tm"), ("qb", 1536, 512, "fm"),
              ("kb", 2048, 512, "tm"), ("vb0", 2560, 512, "tm"), ("vb1", 3072, 512, "tm"),
              ("lr", 3584, 32, "lr"), ("gA", 3616, 512, "fm"), ("gB0", 4128, 512, "tm"), ("gB1", 4640, 512, "tm")]
    for mt in range(NM):
        xT, bxT = xT_ring.next()
        load_x_transposed(g, x_ap, bx, mt, xT, bxT, xs_ring, tp_banks)
        t0 = mt * 512
        for (name, c0, ncol, lay) in blocks:
            wb, bwb = wb_ring.next()
            g.dma(wb[:, :, 0:ncol], w_v[:, :, c0:c0 + ncol], r=[bw], w=[bwb])
            if lay == "fm":
                for fg in range(4):
                    pp, bpp = next_bank()
                    for c in range(8):
                        g.pe(lambda e: e.matmul(pp[:], wb[:, c, fg * 128:(fg + 1) * 128], xT[:, c, :],
                                                start=(c == 0), stop=(c == 7)), r=[bwb, bxT], w=[bpp])
                    fo, bfo = fo_ring.next()
                    if name == "gA":
                        g.act(lambda e: e.activation(fo[:], pp[:], AF.Silu), r=[bpp], w=[bfo])
                    elif fg % 2 == 0:
                        g.act(lambda e: e.activation(fo[:], pp[:], AF.Copy), r=[bpp], w=[bfo])
                    else:
                        g.dve(lambda e: e.tensor_copy(fo[:], pp[:]), r=[bpp], w=[bfo])
                    if name == "qb":
                        g.dma(L["qT_s"][0][mt * 4:(mt + 1) * 4, :, fg, :].rearrange("j k t -> k j t"),
                              fo[:].rearrange("k (j t) -> k j t", j=4), r=[bfo], q="act")
                    elif name == "qa":
                        g.dma(L["qaT_s"][0][fg * 128:(fg + 1) * 128, t0:t0 + 512], fo[:], r=[bfo], q="act")
                    elif name == "ka":
                        g.dma(L["kaT_s"][0][fg * 128:(fg + 1) * 128, HALO + t0:HALO + t0 + 512], fo[:], r=[bfo], q="act")
                    else:
                        g.dma(L["gsaT_s"][0][fg * 128:(fg + 1) * 128, t0:t0 + 512], fo[:], r=[bfo], q="act")
            elif lay == "tm":
                for j in range(4):
                    pp, bpp = next_bank()
                    for c in range(8):
                        g.pe(lambda e: e.matmul(pp[:], xT[:, c, j * 128:(j + 1) * 128], wb[:, c, :],
                                                start=(c == 0), stop=(c == 7)), r=[bwb, bxT], w=[bpp])
                    r0 = t0 + j * 128
                    to, bto = to_ring.next()
                    if name.startswith("gB"):
                        g.act(lambda e: e.activation(to[:], pp[:], AF.Silu), r=[bpp], w=[bto])
                        cc = 0 if name == "gB0" else 512
                        g.dma(L["gs_s"][0][r0:r0 + 128, cc:cc + 512], to[:], r=[bto], q="act")
                    else:
                        g.dve(lambda e: e.tensor_copy(to[:], pp[:]), r=[bpp], w=[bto])
                        if name == "va":
                            g.dma(L["va_s"][0][HALO + r0:HALO + r0 + 128, :], to[:], r=[bto], q="sp")
                        elif name == "kb":
                            g.dma(L["kb_s"][0][r0:r0 + 128, :], to[:], r=[bto], q="sp")
                        else:
                            cc = 0 if name == "vb0" else 512
                            g.dma(L["v_s"][0][r0:r0 + 128, cc:cc + 512], to[:], r=[bto], q="sp")
            else:
                for dd in range(2):
                    pp, bpp = next_bank()
                    for c in range(8):
                        g.pe(lambda e: e.matmul(pp[0:16, :], wb[:, c, dd * 16:(dd + 1) * 16], xT[:, c, :],
                                                start=(c == 0), stop=(c == 7)), r=[bwb, bxT], w=[bpp])
                    la, bla = lra[dd]
                    g.dve(lambda e: e.memset(la[:], 1.0), w=[bla])
                    g.act(lambda e: e.activation(la[0:16, :], pp[0:16, :], AF.Copy), r=[bpp], w=[bla])
                for dd in range(2):
                    la, bla = lra[dd]
                    ua, bua = upa[dd]
                    dst = L["lgf_s"][0] if dd == 0 else L["lgb_s"][0]
                    for j in range(4):
                        pz, bpz = next_bank()
                        g.pe(lambda e: e.matmul(pz[:], la[:, j * 128:(j + 1) * 128], ua[:], start=True, stop=True),
                             r=[bla, bua], w=[bpz])
                        ex, bex = ex_ring.next()
                        g.act(lambda e: e.activation(ex[:], pz[:], AF.Exp, scale=-1.0), r=[bpz], w=[bex])
                        lo, blo = lo_ring.next()
                        g.act(lambda e: e.activation(lo[:], ex[:], AF.Ln, bias=oneb[:]), r=[bex, boneb], w=[blo])
                        r0 = t0 + j * 128
                        g.dma(dst[r0:r0 + 128, :], lo[:], r=[blo], q="act")
    g.end_pass()


PATTERNS = (1, 4, 16)


def t5_bucket(rel):
    half = 16
    max_exact = 8
    n = np.abs(rel)
    large = max_exact + (np.log(np.maximum(n, 1) / max_exact) / np.log(1024 / max_exact) * (half - max_exact)).astype(np.int32)
    large = np.minimum(large, half - 1)
    return np.where(rel > 0, half, 0) + np.where(n < max_exact, n, large)


def attn_tables(rel_bias, left_edge, right_edge):
    rel_bias = np.asarray(rel_bias, dtype=np.float32)
    j = np.arange(64)[:, None]
    i = np.arange(64)[None, :]
    tabs = np.empty((64, 120, 64), np.float32)
    for di, d in enumerate(PATTERNS):
        base = {}
        for delta in (-1, 0, 1):
            off = delta * 64 + j - i
            valid = np.abs(off) <= 64
            bk = t5_bucket(off * d)
            base[delta] = (bk, valid)
        for h in range(8):
            for kind in range(5):
                delta = (-1, 0, 1, -1, 1)[kind]
                bk, valid = base[delta]
                vals = np.where(valid, rel_bias[bk, h], np.float32(NEG))
                if (kind == 3 and left_edge) or (kind == 4 and right_edge):
                    vals = np.full((64, 64), NEG, np.float32)
                tabs[:, (di * 8 + h) * 5 + kind, :] = vals
    return tabs


def pass_attn(g, A):
    T = g.T
    NGp = T // 1024
    g.begin_pass()
    tab, btab = g.sb("tab", [64, 120 * 64], BF16)
    with ExitStack() as es_:
        tabf = es_.enter_context(g.nc.sbuf_tensor(f"tabf_tmp{g._uid}", [64, 120 * 64], F32))
        g._uid += 1
        btabf = Buf("tabf")
        g.dma(tabf[:], A["tabs_ap"].rearrange("j n i -> j (n i)"), w=[btabf])
        g.dve(lambda e: e.tensor_scalar(tab[:], tabf[:], 8.0, None, ALU.mult), r=[btabf], w=[btab])
        g.kb.full_barrier()
    one64, bone64 = g.sb("one64", [64, 64], BF16)
    g.dve(lambda e: e.memset(one64[:], 1.0), w=[bone64])
    k_ring = g.ring("akT", [64, 3072], BF16, 5)
    q_ring = g.ring("aqT", [64, 1024], BF16, 5)
    kd_ring = g.ring("akD", [64, 3072], BF16, 2)
    qd_ring = g.ring("aqD", [64, 1024], BF16, 2)
    v_ring = g.ring("aVd", [64, 48 * 256], BF16, 2)
    gs_ring = g.ring("ags", [64, 1024], BF16, 2)
    accn_ring = g.ring("aaccn", [64, 1024], F32, 4)
    accd_ring = g.ring("aaccd", [64, 1024], F32, 4)
    pt_ring = g.ring("aPT", [64, 192], BF16, 4)
    rec_ring = g.ring("arec", [64, 1024], F32, 2)
    yo_ring = g.ring("ayo", [64, 1024], BF16, 2)
    st_banks = [0, 1, 2, 3]
    st_i = 0
    n_banks = [4, 5]
    d_banks = [6, 7]
    out_i = 0
    qaT_s, kaT_s, va_s = A["qaT_s"][0], A["kaT_s"][0], A["va_s"][0]
    for gi in range(NGp):
        G0 = gi * 1024
        for hq in range(2):
            kts, qts, accs = [], [], []
            for hh in range(4):
                row0 = (hq * 4 + hh) * 64
                kt, bkt = k_ring.next()
                g.dma(kt[:], kaT_s[row0:row0 + 64, G0:G0 + 3072], w=[bkt])
                qt, bqt = q_ring.next()
                g.dma(qt[:], qaT_s[row0:row0 + 64, G0:G0 + 1024], w=[bqt])
                kts.append((kt, bkt))
                qts.append((qt, bqt))
                accs.append((accn_ring.next(), accd_ring.next()))
            for di, d in enumerate(PATTERNS):
                nblk = 1024 // (64 * d)
                nm = nblk + 2
                vd, bvd = v_ring.next()
                tb = G0 + HALO - 64 * d
                rs = min(d, 8)
                for mi in range(nm):
                    for rr0 in range(0, d, rs):
                        src = va_s[tb + mi * 64 * d: tb + (mi + 1) * 64 * d, hq * 256:(hq + 1) * 256]
                        srcv = src.rearrange("(j r) e -> j r e", r=d)[:, rr0:rr0 + rs, :]
                        c0 = (mi * d + rr0) * 256
                        g.dma(vd[:, c0:c0 + rs * 256].rearrange("j (r e) -> j r e", r=rs), srcv, w=[bvd])
                units = [(n, r) for n in range(nblk) for r in range(d)]
                LK = 3072 // d
                LQ = 1024 // d
                for hh in range(4):
                    h = hq * 4 + hh
                    kt, bkt = kts[hh]
                    qt, bqt = qts[hh]
                    if d > 1:
                        kd, bkd = kd_ring.next()
                        g.pool(lambda e: e.tensor_copy(kd[:, :].rearrange("p (r l) -> p r l", r=d),
                                                       kt[:, :].rearrange("p (l r) -> p r l", r=d)), r=[bkt], w=[bkd])
                        qd, bqd = qd_ring.next()
                        g.pool(lambda e: e.tensor_copy(qd[:, :].rearrange("p (r l) -> p r l", r=d),
                                                       qt[:, :].rearrange("p (l r) -> p r l", r=d)), r=[bqt], w=[bqd])
                    else:
                        kd, bkd, qd, bqd = kt, bkt, qt, bqt
                    (accn, baccn), (accd, baccd) = accs[hh]
                    for u0 in range(0, len(units), 8):
                        pN, bN = g.bank[n_banks[out_i % 2]], g.bbank[n_banks[out_i % 2]]
                        pD, bD = g.bank[d_banks[out_i % 2]], g.bbank[d_banks[out_i % 2]]
                        out_i += 1
                        for ui in range(8):
                            n, r = units[u0 + ui]
                            bi2 = st_banks[st_i % 4]
                            st_i += 1
                            pS, bS = g.bank[bi2], g.bbank[bi2]
                            qc = r * LQ + n * 64
                            qv = qd[:, qc:qc + 64]
                            for kb3, delta in enumerate((-1, 0, 1)):
                                m = n + delta
                                if m < 0 and gi == 0:
                                    kind = 3
                                elif m >= nblk and gi == NGp - 1:
                                    kind = 4
                                else:
                                    kind = kb3
                                ti_ = ((di * 8 + h) * 5 + kind) * 64
                                g.pe(lambda e: e.matmul(pS[0:64, kb3 * 64:(kb3 + 1) * 64], g.idb[0:64, 0:64],
                                                        tab[:, ti_:ti_ + 64], start=True, stop=False),
                                     r=[g.bidb, btab], w=[bS])
                                kc = r * LK + (1024 // d) + m * 64
                                kv = kd[:, kc:kc + 64]
                                g.pe(lambda e: e.matmul(pS[0:64, kb3 * 64:(kb3 + 1) * 64], kv, qv,
                                                        start=False, stop=True),
                                     r=[bkd, bqd], w=[bS])
                            pt, bpt = pt_ring.next()
                            g.act(lambda e: e.activation(pt[:], pS[0:64, 0:192], AF.Exp, scale=0.125),
                                  r=[bS], w=[bpt])
                            for kb3, delta in enumerate((-1, 0, 1)):
                                mi = n + delta + 1
                                vcol = (mi * d + r) * 256 + hh * 64
                                g.pe(lambda e: e.matmul(pN[0:64, ui * 64:(ui + 1) * 64], vd[:, vcol:vcol + 64],
                                                        pt[:, kb3 * 64:(kb3 + 1) * 64], start=(kb3 == 0), stop=(kb3 == 2)),
                                     r=[bvd, bpt], w=[bN])
                            for kb3 in range(3):
                                g.pe(lambda e: e.matmul(pD[0:64, ui * 64:(ui + 1) * 64], one64[:, :],
                                                        pt[:, kb3 * 64:(kb3 + 1) * 64], start=(kb3 == 0), stop=(kb3 == 2)),
                                     r=[bone64, bpt], w=[bD])
                        n0, r0_ = units[u0]
                        for (acc, bacc, pO, bO) in ((accn, baccn, pN, bN), (accd, baccd, pD, bD)):
                            views = []
                            if d == 1:
                                views.append((acc[:, n0 * 64:n0 * 64 + 512], pO[0:64, :]))
                            elif d == 4:
                                for nn in range(2):
                                    av = acc[:, (n0 + nn) * 256:(n0 + nn) * 256 + 256].rearrange("p (i r) -> p r i", r=4)
                                    pv = pO[0:64, nn * 256:(nn + 1) * 256].rearrange("p (r i) -> p r i", r=4)
                                    views.append((av, pv))
                            else:
                                av = acc[:, :].rearrange("p (i r) -> p r i", r=16)[:, r0_:r0_ + 8, :]
                                pv = pO[0:64, :].rearrange("p (r i) -> p r i", r=8)
                                views.append((av, pv))
                            for av, pv in views:
                                if di == 0:
                                    g.dve(lambda e: e.tensor_copy(av, pv), r=[bO], w=[bacc])
                                else:
                                    g.dve(lambda e: e.tensor_tensor(av, pv, av, ALU.add), r=[bO, bacc], w=[bacc])
            for hh in range(4):
                h = hq * 4 + hh
                (accn, baccn), (accd, baccd) = accs[hh]
                gs, bgs = gs_ring.next()
                g.dma(gs[:], A["gsaT_s"][0][h * 64:(h + 1) * 64, G0:G0 + 1024], w=[bgs])
                yo, byo = yo_ring.next()
                rec, brec = rec_ring.next()
                g.dve(lambda e: e.reciprocal(rec[:], accd[:]), r=[baccd], w=[brec])
                g.dve(lambda e: e.tensor_tensor(rec[:], rec[:], accn[:], ALU.mult), r=[brec, baccn], w=[brec])
                g.pool(lambda e: e.tensor_tensor(yo[:], rec[:], gs[:], ALU.mult), r=[brec, bgs], w=[byo])
                g.dma(A["yaT_s"][0][h * 64:(h + 1) * 64, G0:G0 + 1024], yo[:], r=[byo], q="pool")
    g.end_pass()


def zero_fill(g, ap2d, rows, cols, dtype):
    with ExitStack() as es:
        z = es.enter_context(g.nc.sbuf_tensor(f"zf_{g._uid}", [128, cols], dtype))
        g._uid += 1
        bz = Buf("zf")
        g.dve(lambda e: e.memset(z[:], 0.0), w=[bz])
        for r0 in range(0, rows, 128):
            n = min(128, rows - r0)
            g.dma(ap2d[r0:r0 + n, :], z[0:n, :], r=[bz])
        g.kb.full_barrier()


def layer_AB(g, x_in, x_out, w_in_ap, w_out_ap, gate_up_ap, gate_bias_ap, gain_ap, tabs_ap, lng_ap, lnb_ap,
             sc=None, zero_halo=True):
    T = g.T
    TE = T + 2 * HALO
    if sc is None:
        sc = {}
    def scr(key, shape, dtype):
        if key not in sc:
            sc[key] = g.dram(f"{key}", shape, dtype)
        return sc[key]
    w_in_bf = scr("ab_w_in_bf", [1024, 5152], BF16)
    wout_bf = scr("wout_bf", [1536, 1024], BF16)
    cast_weight(g, w_in_bf[0], w_in_bf[1], w_in_ap, 1024, 5152)
    cast_weight(g, wout_bf[0], wout_bf[1], w_out_ap, 1536, 1024)
    g.kb.full_barrier()
    L = dict(x_in=x_in, w_in_bf=w_in_bf, gate_up=gate_up_ap, gate_bias=gate_bias_ap,
             qT_s=scr("ab_qT", [T // 128, 128, 4, 128], BF16),
             qaT_s=scr("ab_qaT", [512, T], BF16), kaT_s=scr("ab_kaT", [512, TE], BF16),
             va_s=scr("ab_va", [TE, 512], BF16), gsaT_s=scr("ab_gsaT", [512, T], BF16),
             yaT_s=scr("ab_yaT", [512, T], BF16),
             kb_s=scr("ab_kb", [T, 512], BF16), v_s=scr("ab_vb", [T, 1024], BF16),
             lgf_s=scr("ab_lgf", [T, 512], F32), lgb_s=scr("ab_lgb", [T, 512], F32),
             gs_s=scr("ab_gs", [T, 1024], BF16), of_s=scr("ab_of", [T, 1024], F32))
    if zero_halo:
        ka = L["kaT_s"][0]
        zero_fill(g, ka[:, 0:HALO], 512, HALO, BF16)
        zero_fill(g, ka[:, HALO + T:TE], 512, HALO, BF16)
        va = L["va_s"][0]
        zero_fill(g, va[0:HALO, :], HALO, 512, BF16)
        zero_fill(g, va[HALO + T:TE, :], HALO, 512, BF16)
    stage = 9
    if stage >= 1:
        pass_P_AB(g, L)
    if stage < 2:
        return
    A = dict(qaT_s=L["qaT_s"], kaT_s=L["kaT_s"], va_s=L["va_s"], gsaT_s=L["gsaT_s"], yaT_s=L["yaT_s"],
             tabs_ap=tabs_ap)
    pass_attn(g, A)
    if stage < 3:
        return
    S = dict(H=4, V=256, G=2, cpre="g_", lnscale=math.log(128 ** -0.5), qT_s=L["qT_s"],
             k_s={"f": L["kb_s"], "b": L["kb_s"]}, lg_s={"f": L["lgf_s"], "b": L["lgb_s"]},
             v_s=L["v_s"], of_s=L["of_s"])
    pass_scan(g, S, "f")
    pass_scan(g, S, "b")
    if stage < 4:
        return
    E = dict(H=4, V=256, o_s=L["of_s"], gs_s=L["gs_s"], gain_ap=gain_ap, wout_bf=wout_bf,
             x_in=x_in, x_out=x_out, lng_ap=lng_ap, lnb_ap=lnb_ap, yaT_s=L["yaT_s"])
    pass_epilogue(g, E)


from concourse.bass_utils import run_bass_kernel_spmd

SEQ = 16384
N_CORES = 8
FUSED = False
_PROG = {}


def _declare_inputs(nc, T):
    d = {}
    d["x"] = nc.dram_tensor("x", [T, 1024], F32, kind="ExternalInput").ap()
    d["w_in_ab"] = nc.dram_tensor("w_in_ab", [2, 1024, 5152], F32, kind="ExternalInput").ap()
    d["gup"] = nc.dram_tensor("gup", [2, 2, 16, 512], F32, kind="ExternalInput").ap()
    d["gbias"] = nc.dram_tensor("gbias", [2, 2, 512], F32, kind="ExternalInput").ap()
    d["gnorm"] = nc.dram_tensor("gnorm", [2, 1024], F32, kind="ExternalInput").ap()
    d["w_out_ab"] = nc.dram_tensor("w_out_ab", [2, 1536, 1024], F32, kind="ExternalInput").ap()
    d["w_in_c"] = nc.dram_tensor("w_in_c", [2, 1024, 7680], F32, kind="ExternalInput").ap()
    d["lbr"] = nc.dram_tensor("lbr", [4, 1536], F32, kind="ExternalInput").ap()
    d["hnorm"] = nc.dram_tensor("hnorm", [2, 1536], F32, kind="ExternalInput").ap()
    d["w_out_c"] = nc.dram_tensor("w_out_c", [2, 1536, 1024], F32, kind="ExternalInput").ap()
    d["tabs"] = nc.dram_tensor("tabs", [64, 120, 64], F32, kind="ExternalInput").ap()
    d["lng"] = nc.dram_tensor("lng", [4, 1024], F32, kind="ExternalInput").ap()
    d["lnb"] = nc.dram_tensor("lnb", [4, 1024], F32, kind="ExternalInput").ap()
    return d


def build_program(T, layers):
    key = (T, tuple(layers))
    if key in _PROG:
        return _PROG[key]
    g = Gen(T)
    nc = g.nc
    cst_np, offs = pack_consts()
    d = _declare_inputs(nc, T)
    cst = nc.dram_tensor("cst", list(cst_np.shape), F32, kind="ExternalInput").ap()
    y = nc.dram_tensor("y", [T, 1024], F32, kind="ExternalOutput").ap()
    load_consts(g, cst, offs)
    sc = {}
    pp = [g.dram("xpp0", [T, 1024], F32), g.dram("xpp1", [T, 1024], F32)]
    cur = (d["x"], Buf("x"))
    for li, layer in enumerate(layers):
        nxt = (y, Buf("y")) if li == len(layers) - 1 else pp[li % 2]
        if layer % 2 == 0:
            e = layer // 2
            layer_AB(g, cur, nxt, d["w_in_ab"][e], d["w_out_ab"][e], d["gup"][e], d["gbias"][e], d["gnorm"][e],
                     d["tabs"], d["lng"][layer], d["lnb"][layer], sc=sc)
        else:
            o = layer // 2
            layer_C(g, f"c{layer}", cur, nxt, d["w_in_c"][o], d["w_out_c"][o], d["lbr"], layer, d["hnorm"][o],
                    d["lng"][layer], d["lnb"][layer], sc=sc)
        cur = nxt
    g.kb.finish()
    _PROG[key] = (nc, cst_np)
    return _PROG[key]


def kernel(x, w_in_ab, gla_gate_up, gla_gate_bias, gla_norm, w_out_ab, w_in_c, hgrn_lower_bounds,
           hgrn_norm, w_out_c, rel_bias, ln_gain, ln_bias):
    f = lambda a: np.ascontiguousarray(np.asarray(a), dtype=np.float32)
    x = f(x)
    B, S, _ = x.shape
    assert S == SEQ and B == 2
    tabs = attn_tables(f(rel_bias), True, True)
    common = {"w_in_ab": f(w_in_ab), "gup": f(gla_gate_up), "gbias": f(gla_gate_bias), "gnorm": f(gla_norm),
              "w_out_ab": f(w_out_ab), "w_in_c": f(w_in_c), "lbr": f(hgrn_lower_bounds), "hnorm": f(hgrn_norm),
              "w_out_c": f(w_out_c), "tabs": tabs, "lng": f(ln_gain), "lnb": f(ln_bias)}
    stages = [[0, 1, 2, 3]] if FUSED else [[0], [1], [2], [3]]
    cur = [x[c % B] for c in range(N_CORES)]
    for layers in stages:
        cm = dict(common)
        if not FUSED and layers[0] == 2:
            for k in ("w_in_ab", "gup", "gbias", "gnorm", "w_out_ab"):
                cm[k] = np.ascontiguousarray(cm[k][::-1])
            for k in ("lng", "lnb"):
                cm[k] = np.ascontiguousarray(cm[k][[2, 1, 0, 3]])
            layers = [0]
        nc, cst_np = build_program(S, layers)
        in_maps = []
        for c in range(N_CORES):
            m = dict(cm)
            m["x"] = cur[c]
            m["cst"] = cst_np
            in_maps.append(m)
        res = run_bass_kernel_spmd(nc, in_maps, core_ids=list(range(N_CORES)))
        cur = [np.asarray(res.results[c]["y"], dtype=np.float32) for c in range(N_CORES)]
    return np.stack([cur[0], cur[1]], axis=0)
```
